# Optimizing a Trainium2 kernel written in Bass

```python
import math
import jax, jax.numpy as jnp
from jax import lax
import numpy as np

D_MODEL = 2048
BATCH = 4
SEQ = 4096
DEPTH = 2

HEAD_DIM = 128
MIX_WIDTH = D_MODEL
ATTN_HEADS = (MIX_WIDTH // 2) // HEAD_DIM
ATTN_WIDTH = ATTN_HEADS * HEAD_DIM
GMLP_WIDTH = MIX_WIDTH - ATTN_WIDTH
GMLP_GROUPS = 8
GMLP_GROUP_DIM = GMLP_WIDTH // GMLP_GROUPS
CHUNK = 128
DILATED_BRANCHES = ((128, 1), (512, 4), (2048, 16))
ROPE_THETA = 10000.0
D_FF = 4 * D_MODEL
NORM_EPS = 1e-6
LN_EPS = 1e-5
IN_WIDTH = 3 * ATTN_WIDTH + 2 * GMLP_WIDTH

kernel_name = "hybrid_dilated_attn_gmlp_trunk"


def rms_norm(x, g):
    xf = x.astype(jnp.float32)
    y = xf * lax.rsqrt(jnp.mean(xf * xf, axis=-1, keepdims=True) + NORM_EPS)
    return (y * g.astype(jnp.float32)).astype(x.dtype)


def layer_norm(x, g, b):
    xf = x.astype(jnp.float32)
    mu = jnp.mean(xf, axis=-1, keepdims=True)
    xc = xf - mu
    var = jnp.mean(xc * xc, axis=-1, keepdims=True)
    y = xc * lax.rsqrt(var + LN_EPS) * g.astype(jnp.float32) + b.astype(jnp.float32)
    return y.astype(x.dtype)


def rotary(x, pos):
    half = HEAD_DIM // 2
    inv_freq = ROPE_THETA ** (-jnp.arange(half, dtype=jnp.float32) / half)
    ang = pos.astype(jnp.float32)[:, None] * inv_freq[None, :]
    cos = jnp.cos(ang)[None, :, None, :]
    sin = jnp.sin(ang)[None, :, None, :]
    xf = x.astype(jnp.float32)
    x1, x2 = xf[..., :half], xf[..., half:]
    return jnp.concatenate([x1 * cos - x2 * sin, x2 * cos + x1 * sin], axis=-1).astype(x.dtype)


def dilated_branch(q, k, v, window, dilation):
    B, S, H, Dh = q.shape
    blk = window // dilation
    span = dilation * blk
    s_pad = -(-S // span) * span
    nb = s_pad // span
    pad = ((0, 0), (0, s_pad - S), (0, 0), (0, 0))

    def blocks(t):
        return jnp.pad(t, pad).reshape(B, nb, blk, dilation, H, Dh)

    def with_prev(t):
        prev = jnp.concatenate([jnp.zeros_like(t[:, :1]), t[:, :-1]], axis=1)
        return jnp.concatenate([prev, t], axis=2)

    qb = blocks(q)
    kc = with_prev(blocks(k))
    vc = with_prev(blocks(v))
    scores = jnp.einsum('bnqrhd,bnkrhd->bnrhqk', qb, kc,
                        preferred_element_type=jnp.float32) * (Dh ** -0.5)
    qi = jnp.arange(blk)[:, None]
    kj = jnp.arange(2 * blk)[None, :]
    dist = qi - kj + blk
    key_idx = jnp.arange(nb)[:, None, None] * blk - blk + kj[None]
    valid = (dist >= 0)[None] & (dist <= blk)[None] & (key_idx >= 0)
    scores = jnp.where(valid[None, :, None, None], scores, -jnp.inf)
    m = jnp.max(scores, axis=-1, keepdims=True)
    p = jnp.exp(scores - m)
    l = jnp.sum(p, axis=-1, keepdims=True)
    o = jnp.einsum('bnrhqk,bnkrhd->bnqrhd', p / l, vc.astype(jnp.float32))
    lse = jnp.transpose((m + jnp.log(l))[..., 0], (0, 1, 4, 2, 3))
    o = o.reshape(B, s_pad, H, Dh)[:, :S]
    lse = lse.reshape(B, s_pad, H)[:, :S]
    return o, lse


def dilated_attention(q, k, v):
    outs, lses = [], []
    for window, dilation in DILATED_BRANCHES:
        o, lse = dilated_branch(q, k, v, window, dilation)
        outs.append(o)
        lses.append(lse)
    w = jax.nn.softmax(jnp.stack(lses, axis=0), axis=0)
    return jnp.einsum('nbsh,nbshd->bshd', w, jnp.stack(outs, axis=0))


def chunked_gmlp(u, v, ln_g, ln_b, w_s, b_s):
    B, S, _ = u.shape
    nc = S // CHUNK
    v = layer_norm(v.reshape(B, S, GMLP_GROUPS, GMLP_GROUP_DIM), ln_g, ln_b)
    v = v.reshape(B, nc, CHUNK, GMLP_GROUPS, GMLP_GROUP_DIM)
    causal = jnp.tril(jnp.ones((CHUNK, CHUNK), dtype=w_s.dtype))
    sp = jnp.einsum('gts,bcsgd->bctgd', w_s * causal[None], v) \
        + jnp.transpose(b_s)[None, None, :, :, None]
    out = u.reshape(B, nc, CHUNK, GMLP_GROUPS, GMLP_GROUP_DIM) * sp
    return out.reshape(B, S, GMLP_WIDTH)


def setup_inputs(seed: int = 0) -> dict:
    key = jax.random.key(seed)
    ks = jax.random.split(key, 13)
    f32 = jnp.float32
    nrm = lambda k, shape, s: jax.random.normal(k, shape, f32) * s
    return {
        "x": nrm(ks[0], (BATCH, SEQ, D_MODEL), 1.0),
        "norm1_g": 1.0 + nrm(ks[1], (DEPTH, D_MODEL), 0.02),
        "w_in": nrm(ks[2], (DEPTH, D_MODEL, IN_WIDTH), D_MODEL ** -0.5),
        "gmlp_ln_g": 1.0 + nrm(ks[3], (DEPTH, GMLP_GROUPS, GMLP_GROUP_DIM), 0.02),
        "gmlp_ln_b": nrm(ks[4], (DEPTH, GMLP_GROUPS, GMLP_GROUP_DIM), 0.02),
        "w_spatial": nrm(ks[5], (DEPTH, GMLP_GROUPS, CHUNK, CHUNK), CHUNK ** -0.5),
        "b_spatial": 1.0 + nrm(ks[6], (DEPTH, GMLP_GROUPS, CHUNK), 0.1),
        "w_out": nrm(ks[7], (DEPTH, MIX_WIDTH, D_MODEL), MIX_WIDTH ** -0.5),
        "norm2_g": 1.0 + nrm(ks[8], (DEPTH, D_MODEL), 0.02),
        "w_up": nrm(ks[9], (DEPTH, D_MODEL, D_FF), D_MODEL ** -0.5),
        "w_down": nrm(ks[10], (DEPTH, D_FF, D_MODEL), D_FF ** -0.5),
        "final_g": 1.0 + nrm(ks[11], (D_MODEL,), 0.02),
    }


def reference(x, norm1_g, w_in, gmlp_ln_g, gmlp_ln_b, w_spatial, b_spatial,
              w_out, norm2_g, w_up, w_down, final_g):
    B, S, _ = x.shape
    pos = jnp.arange(S, dtype=jnp.int32)
    splits = [ATTN_WIDTH, 2 * ATTN_WIDTH, 3 * ATTN_WIDTH, 3 * ATTN_WIDTH + GMLP_WIDTH]
    for l in range(DEPTH):
        h = rms_norm(x, norm1_g[l])
        z = h @ w_in[l]
        q, k, v, gu, gv = jnp.split(z, splits, axis=-1)
        heads = (B, S, ATTN_HEADS, HEAD_DIM)
        q = rotary(q.reshape(heads), pos)
        k = rotary(k.reshape(heads), pos)
        attn = dilated_attention(q, k, v.reshape(heads)).reshape(B, S, ATTN_WIDTH)
        gm = chunked_gmlp(jax.nn.gelu(gu), jax.nn.gelu(gv), gmlp_ln_g[l], gmlp_ln_b[l],
                          w_spatial[l], b_spatial[l])
        mix = jnp.concatenate([attn.astype(x.dtype), gm.astype(x.dtype)], axis=-1)
        x = x + mix @ w_out[l]
        h2 = rms_norm(x, norm2_g[l])
        x = x + jnp.square(jax.nn.relu(h2 @ w_up[l])) @ w_down[l]
    return rms_norm(x, final_g)
```

```python
import math
import numpy as np
import concourse.bass as bass
import concourse.mybir as mybir
from concourse.bass_utils import run_bass_kernel_spmd

F32 = mybir.dt.float32
BF16 = mybir.dt.bfloat16
AF = mybir.ActivationFunctionType
ALU = mybir.AluOpType
AX = mybir.AxisListType

D = 2048
SEQ = 4096
W = 4096
TT = 512
NW = 4
SCALE = 1.0 / math.sqrt(128.0)
NEG = -30000.0


class Buf:
    __slots__ = ("name", "w", "r", "excl")

    def __init__(self, name):
        self.name = name
        self.w = {}
        self.r = {}
        self.excl = False


class DSem:
    def __init__(self, sem):
        self.sem = sem
        self.cnt = 0


class Eng:
    def __init__(self, name, sem):
        self.name = name
        self.sem = sem
        self.cnt = 0
        self.seen = {}
        self.q = []


class Tile:
    def __init__(self, name, ap, sem=None):
        self.buf = Buf(name)
        self.ap = ap
        self.sem = sem


class KB:
    def __init__(self, nc, sem_iter):
        self.nc = nc
        self.sem_iter = sem_iter
        self.E = {n: Eng(n, next(sem_iter)) for n in ("pe", "act", "dve", "pool", "sp")}
        self.dsems = []
        self.pool = []
        self.pool_i = 0
        self.nops = 0
        self.cut = None
        self.n_persist = 0

    def dsem(self):
        if self.pool_i < len(self.pool):
            s = self.pool[self.pool_i]
        else:
            s = DSem(next(self.sem_iter))
            self.dsems.append(s)
            self.pool.append(s)
        self.pool_i += 1
        return s

    def new_phase(self):
        self.pool_i = self.n_persist

    def _wait(self, eng, tok, kind):
        sem, val = tok
        if sem is eng.sem:
            if eng.name == "pe":
                return
        k = id(sem)
        if eng.seen.get(k, 0) >= val:
            return
        eng.seen[k] = val
        eng.q.append(lambda e, s=sem, v=val: e.wait_ge(s, v))

    def _deps(self, eng, reads, writes):
        for b in reads:
            for t in b.w.values():
                self._wait(eng, t, "raw")
            if b.excl:
                for t in b.r.values():
                    if t[0] is not eng.sem:
                        self._wait(eng, t, "rar")
        for b in writes:
            for t in b.w.values():
                self._wait(eng, t, "waw")
            for t in b.r.values():
                self._wait(eng, t, "war")

    def _record(self, tok, reads, writes):
        k = id(tok[0])
        for b in reads:
            b.r[k] = tok
        for b in writes:
            b.w = {k: tok}
            b.r = {}

    def op(self, ename, fn, reads=(), writes=(), signal=True):
        eng = self.E[ename]
        self.nops += 1
        if self.cut is not None and self.nops > self.cut:
            return None
        self._deps(eng, reads, writes)
        if signal:
            eng.cnt += 1
            tok = (eng.sem, eng.cnt)
            eng.q.append(lambda e, f=fn, s=eng.sem: f(e).then_inc(s, 1))
        else:
            tok = (eng.sem, eng.cnt + 1)
            eng.q.append(lambda e, f=fn: f(e))
        self._record(tok, reads, writes)
        return tok

    def dma(self, qname, out, in_, sem, reads=(), writes=()):
        eng = self.E[qname]
        self.nops += 1
        if self.cut is not None and self.nops > self.cut:
            return None
        self._deps(eng, reads, writes)
        sem.cnt += 16
        tok = (sem.sem, sem.cnt)
        eng.q.append(lambda e, o=out, i=in_, s=sem.sem: e.dma_start(out=o, in_=i).then_inc(s, 16))
        self._record(tok, reads, writes)
        return tok

    def barrier(self, engines=("pe", "act", "dve", "pool", "sp"), final=False):
        if self.cut is not None and self.nops > self.cut and not final:
            return
        toks = [(e.sem, e.cnt) for e in self.E.values() if e.cnt > 0]
        toks += [(s.sem, s.cnt) for s in self.dsems if s.cnt > 0 and not getattr(s, "is_w", False)]
        for en in engines:
            eng = self.E[en]
            for t in toks:
                if t[0] is eng.sem:
                    continue
                self._wait(eng, t, "raw")


class Arena:
    def __init__(self, ap_f32, nwords):
        self.ap = ap_f32
        self.n = nwords
        self.top = 0

    def mark(self):
        return self.top

    def reset(self, m):
        self.top = m

    def _take(self, words):
        words = (words + 7) // 8 * 8
        o = self.top
        self.top += words
        assert self.top <= self.n, f"arena overflow {self.top} > {self.n}"
        return o

    def f32(self, shape):
        n = int(np.prod(shape))
        o = self._take(n)
        ap = self.ap[:, o:o + n]
        return self._shape(ap, shape)

    def bf16(self, shape):
        n = int(np.prod(shape))
        o = self._take((n + 1) // 2)
        ap = self.ap[:, o:o + (n + 1) // 2].bitcast(BF16)[:, 0:n]
        return self._shape(ap, shape)

    @staticmethod
    def _shape(ap, shape):
        if len(shape) == 1:
            return ap
        if len(shape) == 2:
            return ap.rearrange("p (a b) -> p a b", b=shape[1])
        if len(shape) == 3:
            return ap.rearrange("p (a b c) -> p a b c", b=shape[1], c=shape[2])
        raise ValueError(shape)


def build_program(nlayers=2, dbg=False, upto=None, cut=None):
    nc = bass.Bass("TRN2", target_bir_lowering=False)

    def din(name, shape, dt=F32):
        return nc.dram_tensor(name, list(shape), dt, kind="ExternalInput")

    xT_in = din("xT", [128, 16, W]).ap()
    w_in = din("w_in", [2, D, 5120])
    w_out = din("w_out", [2, D, D])
    w_up = din("w_up", [2, D, 4 * D])
    w_down = din("w_down", [2, 4 * D, D])
    gT_in = din("gT", [128, 5, 16]).ap()
    ln_in = din("lnbc", [128, 2, 2, 1024]).ap()
    bs_in = din("bs4", [128, 2, 8, 512]).ap()
    ws_in = din("wsT", [128, 2, 8, 128]).ap()
    cos_in = din("cosT", [128, W]).ap()
    sin_in = din("sinT", [128, W]).ap()
    cst_in = din("cst", [128, 4, 128]).ap()
    mb_in = din("mb", [128, 2, 256]).ap()
    yT = nc.dram_tensor("yT", [128, 16, 2048], F32, kind="ExternalOutput").ap()

    skind = "ExternalOutput" if dbg else "Internal"
    xs = nc.dram_tensor("xs", [128, 16, W], F32, kind=skind).ap()
    qT_s = nc.dram_tensor("qT_s", [128, 8, W], BF16, kind=skind).ap()
    kT_s = nc.dram_tensor("kT_s", [128, 8, W], BF16, kind=skind).ap()
    v_s = nc.dram_tensor("v_s", [8, W, 128], BF16, kind=skind).ap()
    gu_s = nc.dram_tensor("gu_s", [128, 8, W], BF16, kind=skind).ap()
    gv_s = nc.dram_tensor("gv_s", [W, 1024], BF16, kind=skind).ap()
    mix_s = nc.dram_tensor("mix_s", [128, 16, W], BF16, kind=skind).ap()

    NWORDS = 53200
    NSEM = 40
    import contextlib
    with contextlib.ExitStack() as es:
        arena_t = es.enter_context(nc.sbuf_tensor("arena", [128, NWORDS], F32))
        banks = [es.enter_context(nc.psum_tensor(f"ps{i}", [128, 512], F32)) for i in range(8)]
        sems = [es.enter_context(nc.semaphore(f"s{i}")) for i in range(NSEM)]
        block = es.enter_context(nc.Block())
        kb = KB(nc, iter(sems))
        kb.cut = cut
        A = Arena(arena_t[:, :], NWORDS)

        def T(name, ap, dma=False):
            return Tile(name, ap, kb.dsem() if dma else None)

        ps_tiles = [Tile(f"ps{i}", banks[i][:, :]) for i in range(8)]
        for t_ in ps_tiles:
            t_.buf.excl = True
        ps_i = [0]

        def psum():
            t = ps_tiles[ps_i[0] % 8]
            ps_i[0] += 1
            return t

        wslots = [T(f"w{i}", A.bf16([16, 512]), dma=True) for i in range(NW)]
        for s in wslots:
            s.sem.is_w = True
        ones_bf = A.bf16([128])
        ident_bf = A.bf16([128])
        rmat_bf = A.bf16([128])
        tril_f = A.f32([128])
        mb_bf = A.bf16([2, 256])
        gT = A.f32([5, 16])
        eps6 = A.f32([1])
        eps5 = A.f32([1])
        cst_f = T("cst_f", A.f32([4, 128]), dma=True)
        mb_f = T("mb_f", A.f32([2, 256]), dma=True)
        gT_t = T("gT", gT, dma=True)
        PERSIST = A.mark()
        kb.n_persist = kb.pool_i

        kb.dma("sp", cst_f.ap, cst_in, cst_f.sem, writes=[cst_f.buf])
        kb.dma("sp", mb_f.ap, mb_in, mb_f.sem, writes=[mb_f.buf])
        kb.dma("sp", gT_t.ap, gT_in, gT_t.sem, writes=[gT_t.buf])
        cbuf = Buf("consts")
        kb.op("dve", lambda e: e.memset(ones_bf, 1.0), writes=[cbuf])
        kb.op("dve", lambda e: e.memset(eps6, 1e-6), writes=[cbuf])
        kb.op("dve", lambda e: e.memset(eps5, 1e-5), writes=[cbuf])
        kb.op("dve", lambda e: e.tensor_copy(out=ident_bf, in_=cst_f.ap[:, 0, :]), reads=[cst_f.buf], writes=[cbuf])
        kb.op("dve", lambda e: e.tensor_copy(out=rmat_bf, in_=cst_f.ap[:, 1, :]), reads=[cst_f.buf], writes=[cbuf])
        kb.op("dve", lambda e: e.tensor_copy(out=tril_f, in_=cst_f.ap[:, 2, :]), reads=[cst_f.buf], writes=[cbuf])
        kb.op("dve", lambda e: e.tensor_copy(out=mb_bf, in_=mb_f.ap), reads=[mb_f.buf], writes=[cbuf])
        kb.barrier()

        def wsrc(kind, l, c):
            if kind == "in":
                return w_in[l].rearrange("(kc p) n -> p kc n", p=128)[:, :, c * 512:(c + 1) * 512]
            if kind == "out":
                return w_out[l].rearrange("(kc p) n -> p kc n", p=128)[:, :, c * 512:(c + 1) * 512]
            if kind == "up":
                return w_up[l].rearrange("(kc p) n -> p kc n", p=128)[:, :, c * 512:(c + 1) * 512]
            J, cc = c
            return w_down[l][J * 2048:(J + 1) * 2048].rearrange("(jc p) n -> p jc n", p=128)[:, :, cc * 512:(cc + 1) * 512]

        def p1_chunks(full):
            return list(range(10)) if full else [2, 3, 4, 5]

        plan = []
        for l in range(nlayers):
            for t in range(8):
                plan.append(("P1", l, t, (l == 0) or (t >= 4)))
            plan.append(("P2", l))
            plan.append(("P3", l))
            for t in (range(8) if (l == 0 and nlayers > 1) else range(4, 8)):
                plan.append(("P4", l, t))
        if upto is not None:
            plan = plan[:upto]
        wsched = []
        for ph in plan:
            if ph[0] == "P1":
                wsched += [("in", ph[1], c) for c in p1_chunks(ph[3])]
            elif ph[0] == "P4":
                l = ph[1]
                wsched += [("out", l, c) for c in range(4)]
                for J in range(4):
                    wsched += [("up", l, J * 4 + c) for c in range(4)]
                    wsched += [("down", l, (J, cc)) for cc in range(4)]
        wst = {"issued": 0, "used": 0}

        def next_w(expect):
            i = wst["used"]
            assert wsched[i] == expect, (wsched[i], expect)
            wst["used"] += 1
            while wst["issued"] < min(len(wsched), i + NW):
                j = wst["issued"]
                slot = wslots[j % NW]
                kb.dma("pool", slot.ap, wsrc(*wsched[j]), slot.sem, writes=[slot.buf])
                wst["issued"] += 1
            return wslots[i % NW]

        def mm(out, lhsT, rhs, start, stop, reads, writes, signal):
            kb.op("pe", lambda e: e.matmul(out, lhsT=lhsT, rhs=rhs, start=start, stop=stop),
                  reads=reads, writes=writes, signal=signal)

        def rms_norm(xt, sq, rt, rstd, g_ap, out_ap, out_buf):
            kb.op("act", lambda e: e.activation(out=sq.ap, in_=xt.ap, func=AF.Square), reads=[xt.buf], writes=[sq.buf])
            ps = psum()
            for fc in range(16):
                mm(ps.ap, ones_bf, sq.ap[:, fc, :], fc == 0, fc == 15, [sq.buf], [ps.buf], fc == 15)
            kb.op("act", lambda e: e.activation(out=rt.ap, in_=ps.ap, func=AF.Sqrt, bias=eps6, scale=1.0 / D),
                  reads=[ps.buf], writes=[rt.buf])
            kb.op("dve", lambda e: e.reciprocal(out=rstd.ap, in_=rt.ap), reads=[rt.buf], writes=[rstd.buf])
            inplace = out_buf is xt.buf
            for fc in range(16):
                edge = fc in (0, 15)
                kb.op("dve", lambda e, fc=fc: e.scalar_tensor_tensor(
                    out=out_ap[:, fc, :], in0=xt.ap[:, fc, :], scalar=g_ap[:, fc:fc + 1], in1=rstd.ap,
                    op0=ALU.mult, op1=ALU.mult), reads=([] if (inplace and not edge) else [xt.buf]) + [rstd.buf],
                    writes=[out_buf] if edge else [])

        def gelu(ps, x2s, gu_, gs, out_ap, out_buf):
            kb.op("act", lambda e: e.activation(out=x2s.ap, in_=ps.ap, func=AF.Square, scale=math.sqrt(0.044715)),
                  reads=[ps.buf], writes=[x2s.buf])
            kb.op("dve", lambda e: e.scalar_tensor_tensor(out=gu_.ap, in0=x2s.ap, scalar=1.0, in1=ps.ap, op0=ALU.add, op1=ALU.mult),
                  reads=[x2s.buf, ps.buf], writes=[gu_.buf])
            kb.op("act", lambda e: e.activation(out=gs.ap, in_=gu_.ap, func=AF.Sigmoid, scale=1.5957691216057308),
                  reads=[gu_.buf], writes=[gs.buf])
            kb.op("dve", lambda e: e.tensor_tensor(out=out_ap, in0=gs.ap, in1=ps.ap, op=ALU.mult), reads=[gs.buf, ps.buf], writes=[out_buf])

        def phase_P1(l):
            A.reset(PERSIST)
            kb.new_phase()
            xt = T("xt", A.f32([16, 512]), dma=True)
            sq = T("sq", A.bf16([16, 512]))
            hTs = [T(f"hT{i}", A.bf16([16, 512])) for i in range(2)]
            cos = T("cos", A.f32([512]), dma=True)
            sin = T("sin", A.f32([512]), dma=True)
            lnp = T("lnp", A.f32([2, 1024]), dma=True)
            rt = T("rt", A.f32([512]))
            rstd = T("rstd", A.f32([512]))
            zb = [T(f"zb{i}", A.bf16([512])) for i in range(2)]
            t1s = [T(f"t1{i}", A.f32([512])) for i in range(3)]
            t2s = [T(f"t2{i}", A.f32([512])) for i in range(3)]
            qk_st = [T(f"qkst{i}", A.bf16([4, 512]), dma=True) for i in range(2)]
            v_st = [T(f"vst{i}", A.bf16([512]), dma=True) for i in range(2)]
            gv_st = [T(f"gvst{i}", A.bf16([512]), dma=True) for i in range(2)]
            NSET = 5
            gvws = [T(f"gvw{i}", A.f32([512])) for i in range(NSET)]
            sqvs = [T(f"sqv{i}", A.f32([512])) for i in range(2)]
            st4s = [[T(f"st4{i}{j}", A.f32([4])) for i in range(5)] for j in range(NSET)]
            pendB = []
            ctr = {"zb": 0, "st": 0, "v": 0, "gv": 0, "g": 0, "t": 0, "w": 0}
            kb.dma("sp", lnp.ap, ln_in[:, l, :, :], lnp.sem, writes=[lnp.buf])
            xsrc = xT_in if l == 0 else xs
            pend = []

            post = []

            def run_pending(keep=0):
                posts_now = post[:]
                del post[:]
                while len(pend) > keep:
                    r = pend.pop(0)()
                    if r is not None:
                        post.append(r)
                for p_ in posts_now:
                    p_()

            xt_loaded = {}

            def load_xt(t):
                if t < 8 and t not in xt_loaded:
                    xt_loaded[t] = True
                    kb.dma("sp", xt.ap, xsrc[:, :, t * TT:(t + 1) * TT], xt.sem, writes=[xt.buf])

            prepped = {}

            def prep(t):
                if t >= 8 or t in prepped:
                    return
                prepped[t] = True
                load_xt(t)
                h_ = hTs[t % 2]
                rms_norm(xt, sq, rt, rstd, gT[:, l, :], h_.ap, h_.buf)

            def tile_fn(t, full):
                tok0 = t * TT
                prep(t)
                load_xt(t + 1)
                hT = hTs[t % 2]
                kb.dma("sp", cos.ap, cos_in[:, tok0:tok0 + TT], cos.sem, writes=[cos.buf])
                kb.dma("sp", sin.ap, sin_in[:, tok0:tok0 + TT], sin.sem, writes=[sin.buf])
                chunks = p1_chunks(full)
                for ci, c in enumerate(chunks):
                    if ci == len(chunks) // 2:
                        prep(t + 1)
                    w = next_w(("in", l, c))
                    if c in (0, 1, 2, 3, 6, 7):
                        st = qk_st[ctr["st"] % 2]
                        ctr["st"] += 1
                        for cb in range(4):
                            ps = psum()
                            for kc in range(16):
                                mm(ps.ap, w.ap[:, kc, cb * 128:(cb + 1) * 128], hT.ap[:, kc, :], kc == 0, kc == 15,
                                   [w.buf, hT.buf], [ps.buf], kc == 15)
                            run_pending()
                            if c < 4:
                                def evac(ps=ps, st=st, cb=cb):
                                    z = zb[ctr["zb"] % 2]
                                    ctr["zb"] += 1
                                    t1 = t1s[ctr["t"] % 3]
                                    t2 = t2s[ctr["t"] % 3]
                                    ctr["t"] += 1
                                    kb.op("act", lambda e: e.activation(out=z.ap, in_=ps.ap, func=AF.Copy),
                                          reads=[ps.buf], writes=[z.buf])
                                    ps2 = psum()
                                    mm(ps2.ap, rmat_bf, z.ap, True, True, [z.buf], [ps2.buf], True)
                                    kb.op("dve", lambda e: e.tensor_tensor(out=t1.ap, in0=ps.ap, in1=cos.ap, op=ALU.mult),
                                          reads=[ps.buf, cos.buf], writes=[t1.buf])
                                    kb.op("dve", lambda e: e.tensor_tensor(out=t2.ap, in0=ps2.ap, in1=sin.ap, op=ALU.mult),
                                          reads=[ps2.buf, sin.buf], writes=[t2.buf])
                                    def add_(t1=t1, t2=t2):
                                        kb.op("dve", lambda e: e.tensor_tensor(out=st.ap[:, cb, :], in0=t1.ap, in1=t2.ap, op=ALU.add),
                                              reads=[t1.buf, t2.buf], writes=[st.buf])
                                    return add_
                            else:
                                def evac(ps=ps, st=st, cb=cb):
                                    kb.op("act", lambda e: e.activation(out=st.ap[:, cb, :], in_=ps.ap, func=AF.Gelu_apprx_tanh),
                                          reads=[ps.buf], writes=[st.buf])
                            if cb == 3:
                                dst = {0: qT_s, 1: qT_s, 2: kT_s, 3: kT_s, 6: gu_s, 7: gu_s}[c]
                                h0 = (c % 2) * 4

                                def evac_last(evac=evac, st=st, dst=dst, h0=h0):
                                    r = evac()

                                    def fin():
                                        if r is not None:
                                            r()
                                        kb.dma("sp", dst[:, h0:h0 + 4, tok0:tok0 + TT], st.ap, st.sem, reads=[st.buf])
                                    return fin
                                pend.append(evac_last)
                            else:
                                pend.append(evac)
                    else:
                        for tb in range(4):
                            ps = psum()
                            for kc in range(16):
                                mm(ps.ap, hT.ap[:, kc, tb * 128:(tb + 1) * 128], w.ap[:, kc, :], kc == 0, kc == 15,
                                   [w.buf, hT.buf], [ps.buf], kc == 15)
                            run_pending()
                            r0 = tok0 + tb * 128
                            if c in (4, 5):
                                def evac(ps=ps, r0=r0, c=c):
                                    vs = v_st[ctr["v"] % 2]
                                    ctr["v"] += 1
                                    kb.op("act", lambda e: e.activation(out=vs.ap, in_=ps.ap, func=AF.Copy),
                                          reads=[ps.buf], writes=[vs.buf])
                                    h0 = (c - 4) * 4
                                    kb.dma("sp", v_s[h0:h0 + 4, r0:r0 + 128, :].rearrange("h t d -> t h d"),
                                           vs.ap.rearrange("p (h d) -> p h d", d=128), vs.sem, reads=[vs.buf])
                            else:
                                def evac(ps=ps, r0=r0, c=c, tb=tb):
                                    j = ctr["w"] % NSET
                                    ctr["w"] += 1
                                    gw = gvws[j]
                                    sqv = sqvs[j % 2]
                                    sm, mean, vsum, sd, rs = st4s[j]
                                    kb.op("act", lambda e: e.activation(out=gw.ap, in_=ps.ap, func=AF.Gelu_apprx_tanh),
                                          reads=[ps.buf], writes=[gw.buf])
                                    kb.op("act", lambda e: e.activation(out=sqv.ap, in_=gw.ap, func=AF.Square), reads=[gw.buf], writes=[sqv.buf])
                                    x3 = gw.ap.rearrange("p (g d) -> p g d", d=128)
                                    s3 = sqv.ap.rearrange("p (g d) -> p g d", d=128)
                                    kb.op("dve", lambda e: e.tensor_reduce(out=sm.ap, in_=x3, axis=AX.X, op=ALU.add),
                                          reads=[gw.buf], writes=[sm.buf])
                                    kb.op("dve", lambda e: e.tensor_reduce(out=vsum.ap, in_=s3, axis=AX.X, op=ALU.add),
                                          reads=[sqv.buf], writes=[vsum.buf])
                                    kb.op("dve", lambda e: e.tensor_scalar(out=mean.ap, in0=sm.ap, scalar1=1.0 / 128, scalar2=None, op0=ALU.mult),
                                          reads=[sm.buf], writes=[mean.buf])
                                    kb.op("dve", lambda e: e.tensor_tensor(out=sm.ap, in0=mean.ap, in1=mean.ap, op=ALU.mult),
                                          reads=[mean.buf], writes=[sm.buf])
                                    kb.op("dve", lambda e: e.scalar_tensor_tensor(out=vsum.ap, in0=vsum.ap, scalar=1.0 / 128, in1=sm.ap,
                                                                                  op0=ALU.mult, op1=ALU.subtract),
                                          reads=[vsum.buf, sm.buf], writes=[vsum.buf])

                                    def b1():
                                        kb.op("act", lambda e: e.activation(out=sd.ap, in_=vsum.ap, func=AF.Sqrt, bias=eps5, scale=1.0),
                                              reads=[vsum.buf], writes=[sd.buf])

                                    def b2():
                                        kb.op("dve", lambda e: e.reciprocal(out=rs.ap, in_=sd.ap), reads=[sd.buf], writes=[rs.buf])
                                        kb.op("dve", lambda e: e.scalar_tensor_tensor(out=sm.ap, in0=mean.ap, scalar=-1.0, in1=rs.ap,
                                                                                      op0=ALU.mult, op1=ALU.mult),
                                              reads=[mean.buf, rs.buf], writes=[sm.buf])
                                        c0 = (c - 8) * 512
                                        for gg in range(4):
                                            edge = gg in (0, 3)
                                            kb.op("dve", lambda e, gg=gg: e.tensor_scalar(
                                                out=gw.ap[:, gg * 128:(gg + 1) * 128], in0=gw.ap[:, gg * 128:(gg + 1) * 128],
                                                scalar1=rs.ap[:, gg:gg + 1], scalar2=sm.ap[:, gg:gg + 1], op0=ALU.mult, op1=ALU.add),
                                                reads=[rs.buf, sm.buf] + ([gw.buf] if edge else []), writes=[gw.buf] if edge else [])
                                        kb.op("dve", lambda e: e.tensor_tensor(out=gw.ap, in0=gw.ap, in1=lnp.ap[:, 0, c0:c0 + 512], op=ALU.mult),
                                              reads=[gw.buf, lnp.buf], writes=[gw.buf])
                                        gst = gv_st[ctr["gv"] % 2]
                                        ctr["gv"] += 1
                                        kb.op("dve", lambda e: e.tensor_tensor(out=gst.ap, in0=gw.ap, in1=lnp.ap[:, 1, c0:c0 + 512], op=ALU.add),
                                              reads=[gw.buf, lnp.buf], writes=[gst.buf])
                                        kb.dma("sp", gv_s[r0:r0 + 128, c0:c0 + 512], gst.ap, gst.sem, reads=[gst.buf])
                                    pendB.append((b1, b2))
                                    if tb == 3:
                                        def flushB():
                                            bs_ = pendB[:]
                                            del pendB[:]
                                            for x1_, _ in bs_:
                                                x1_()
                                            for _, x2_ in bs_:
                                                x2_()
                                        return flushB
                            pend.append(evac)
                run_pending()
                run_pending()
            return tile_fn

        def phase_P2(l):
            A.reset(PERSIST)
            kb.new_phase()
            halves = [0, 1] if l == 0 else [1]
            hb = []
            for i in range(2):
                hb.append(dict(
                    QT=T(f"QT{i}", A.bf16([W]), dma=True), KT=T(f"KT{i}", A.bf16([W]), dma=True),
                    V1=T(f"V1{i}", A.bf16([32, 128]), dma=True), V2=T(f"V2{i}", A.bf16([32, 128]), dma=True),
                    V3=T(f"V3{i}", A.bf16([32, 128]), dma=True)))
            accs = [T(f"acc{i}", A.f32([2, 2048])) for i in range(2)]
            rls = [T("rl0", A.f32([2048]))] * 2
            pipe = []
            LAG = 4
            pTs = [T(f"pT{i}", A.bf16([256])) for i in range(7)]
            mos = [T(f"mo{i}", A.bf16([2048]), dma=True) for i in range(2)]
            ctr = {"pT": 0, "mo": 0}

            def load(h):
                b = hb[h % 2]
                kb.dma("sp", b["QT"].ap, qT_s[:, h, :], b["QT"].sem, writes=[b["QT"].buf])
                kb.dma("sp", b["KT"].ap, kT_s[:, h, :], b["KT"].sem, writes=[b["KT"].buf])
                kb.dma("sp", b["V1"].ap, v_s[h].rearrange("(b j) d -> j b d", j=128), b["V1"].sem, writes=[b["V1"].buf])
                kb.dma("sp", b["V2"].ap.rearrange("p (m r) d -> p m r d", r=4),
                       v_s[h].rearrange("(m j r) d -> j m r d", j=128, r=4), b["V2"].sem, writes=[b["V2"].buf])
                kb.dma("sp", b["V3"].ap.rearrange("p (m r) d -> p m r d", r=16),
                       v_s[h].rearrange("(m j r) d -> j m r d", j=128, r=16), b["V3"].sem, writes=[b["V3"].buf])

            load(0)
            for h in range(8):
                b = hb[h % 2]
                QT, KT = b["QT"], b["KT"]
                for H in halves:
                    base = H * 2048
                    acc = accs[ctr["mo"] % 2]
                    rl = rls[ctr["mo"] % 2]
                    blocks = []
                    for n in range(16):
                        s0 = base + n * 128
                        prev = None
                        if not (H == 0 and n == 0):
                            prev = (KT.ap[:, s0 - 128:s0], b["V1"], s0 // 128 - 1, 1 if n == 0 else 0)
                        blocks.append((True, QT.ap[:, s0:s0 + 128], acc.ap[:, :, n * 128:(n + 1) * 128],
                                       (KT.ap[:, s0:s0 + 128], b["V1"], s0 // 128), prev))
                    for m in range(4):
                        for r in range(4):
                            s0 = base + m * 512
                            prev = None
                            if not (H == 0 and m == 0):
                                prev = (KT.ap[:, s0 - 512 + r:s0:4], b["V2"], (s0 // 512 - 1) * 4 + r, 1 if m == 0 else 0)
                            blocks.append((False, QT.ap[:, s0 + r:s0 + 512:4], acc.ap[:, :, m * 512 + r:(m + 1) * 512:4],
                                           (KT.ap[:, s0 + r:s0 + 512:4], b["V2"], (s0 // 512) * 4 + r), prev))
                    for r in range(16):
                        prev = None
                        if H == 1:
                            prev = (KT.ap[:, r:2048:16], b["V3"], r, 1)
                        blocks.append((False, QT.ap[:, base + r:base + 2048:16], acc.ap[:, :, r:2048:16],
                                       (KT.ap[:, base + r:base + 2048:16], b["V3"], H * 16 + r), prev))
                    def stage1(blk):
                        (first, q_ap, acc_v, cur, prev) = blk
                        sc = psum()
                        rd = [QT.buf, KT.buf]
                        if prev is not None:
                            mbp = mb_bf[:, prev[3], 0:128]
                            mm(sc.ap[:, 0:128], ident_bf, mbp, True, False, [], [sc.buf], False)
                            mm(sc.ap[:, 0:128], prev[0], q_ap, False, True, rd, [sc.buf], False)
                        mm(sc.ap[:, 128:256], ident_bf, mb_bf[:, 0, 128:256], True, False, [], [sc.buf], False)
                        mm(sc.ap[:, 128:256], cur[0], q_ap, False, True, rd, [sc.buf], True)
                        lo = 0 if prev is not None else 128
                        pT = pTs[ctr["pT"] % len(pTs)]
                        ctr["pT"] += 1
                        kb.op("act", lambda e, pT=pT, sc=sc, lo=lo: e.activation(out=pT.ap[:, lo:256], in_=sc.ap[:, lo:256], func=AF.Exp, scale=SCALE),
                              reads=[sc.buf], writes=[pT.buf])
                        return pT

                    def stage2(blk, pT, acc=acc):
                        (first, q_ap, acc_v, cur, prev) = blk
                        ol = psum()
                        vc = cur[1].ap[:, cur[2], :]
                        if prev is not None:
                            vp = prev[1].ap[:, prev[2], :]
                            mm(ol.ap[:, 0:128], vp, pT.ap[:, 0:128], True, False, [pT.buf, prev[1].buf], [ol.buf], False)
                            mm(ol.ap[:, 0:128], vc, pT.ap[:, 128:256], False, True, [pT.buf, cur[1].buf], [ol.buf], False)
                            mm(ol.ap[:, 128:256], ones_bf, pT.ap[:, 0:128], True, False, [pT.buf], [ol.buf], False)
                            mm(ol.ap[:, 128:256], ones_bf, pT.ap[:, 128:256], False, True, [pT.buf], [ol.buf], True)
                        else:
                            mm(ol.ap[:, 0:128], vc, pT.ap[:, 128:256], True, True, [pT.buf, cur[1].buf], [ol.buf], False)
                            mm(ol.ap[:, 128:256], ones_bf, pT.ap[:, 128:256], True, True, [pT.buf], [ol.buf], True)
                        olv = ol.ap[:, 0:256].rearrange("p (c q) -> p c q", c=2)
                        if first:
                            kb.op("act", lambda e, acc_v=acc_v, olv=olv: e.activation(out=acc_v, in_=olv, func=AF.Copy),
                                  reads=[ol.buf], writes=[acc.buf])
                        else:
                            kb.op("dve", lambda e, acc_v=acc_v, olv=olv: e.tensor_tensor(out=acc_v, in0=acc_v, in1=olv, op=ALU.add),
                                  reads=[ol.buf, acc.buf], writes=[acc.buf])

                    mo = mos[ctr["mo"] % 2]
                    ctr["mo"] += 1

                    def fin(acc=acc, rl=rl, mo=mo, h=h, base=base):
                        kb.op("act", lambda e: e.activation(out=rl.ap, in_=acc.ap[:, 1, :], func=AF.Ln), reads=[acc.buf], writes=[rl.buf])
                        kb.op("act", lambda e: e.activation(out=rl.ap, in_=rl.ap, func=AF.Exp, scale=-1.0), reads=[rl.buf], writes=[rl.buf])
                        kb.op("dve", lambda e: e.tensor_tensor(out=mo.ap, in0=acc.ap[:, 0, :], in1=rl.ap, op=ALU.mult),
                              reads=[acc.buf, rl.buf], writes=[mo.buf])
                        kb.dma("sp", mix_s[:, h, base:base + 2048], mo.ap, mo.sem, reads=[mo.buf])

                    for bi, blk in enumerate(blocks):
                        pipe.append((stage2, blk, stage1(blk), fin if bi == len(blocks) - 1 else None))
                        if len(pipe) > LAG:
                            s2, b_, p_, f_ = pipe.pop(0)
                            s2(b_, p_)
                            if f_ is not None:
                                f_()
                        if bi == LAG + 1 and H == halves[0] and h + 1 < 8:
                            load(h + 1)
            while pipe:
                s2, b_, p_, f_ = pipe.pop(0)
                s2(b_, p_)
                if f_ is not None:
                    f_()

        def phase_P3(l):
            A.reset(PERSIST)
            kb.new_phase()
            halves = [0, 1] if l == 0 else [1]
            ws_f = T("ws_f", A.f32([8, 128]), dma=True)
            wm = T("wm", A.bf16([8, 128]))
            bs4 = T("bs4", A.f32([8, 512]), dma=True)
            gvt = T("gvt", A.bf16([16, 1024]), dma=True)
            guT = T("guT", A.bf16([8, 2048]), dma=True)
            gm = T("gm", A.bf16([8, 2048]), dma=True)
            tmps = [T(f"tmp{i}", A.f32([512])) for i in range(2)]
            kb.dma("sp", ws_f.ap, ws_in[:, l, :, :], ws_f.sem, writes=[ws_f.buf])
            kb.dma("sp", bs4.ap, bs_in[:, l, :, :], bs4.sem, writes=[bs4.buf])
            for g in range(8):
                kb.op("dve", lambda e, g=g: e.tensor_tensor(out=wm.ap[:, g, :], in0=ws_f.ap[:, g, :], in1=tril_f, op=ALU.mult),
                      reads=[ws_f.buf], writes=[wm.buf])
            k = 0
            for H in halves:
                base = H * 2048
                kb.dma("sp", gvt.ap, gv_s[base:base + 2048, :].rearrange("(c s) f -> s c f", s=128), gvt.sem, writes=[gvt.buf])
                kb.dma("sp", guT.ap, gu_s[:, :, base:base + 2048], guT.sem, writes=[guT.buf])
                for g in range(8):
                    for c4 in range(4):
                        ps = psum()
                        for cc in range(4):
                            c = c4 * 4 + cc
                            mm(ps.ap[:, cc * 128:(cc + 1) * 128], gvt.ap[:, c, g * 128:(g + 1) * 128], wm.ap[:, g, :], True, True,
                               [gvt.buf, wm.buf], [ps.buf], cc == 3)
                        tmp = tmps[k % 2]
                        k += 1
                        kb.op("dve", lambda e, tmp=tmp, ps=ps, g=g: e.tensor_tensor(out=tmp.ap, in0=ps.ap, in1=bs4.ap[:, g, :], op=ALU.add),
                              reads=[ps.buf, bs4.buf], writes=[tmp.buf])
                        kb.op("dve", lambda e, tmp=tmp, g=g, c4=c4: e.tensor_tensor(
                            out=gm.ap[:, g, c4 * 512:(c4 + 1) * 512], in0=tmp.ap, in1=guT.ap[:, g, c4 * 512:(c4 + 1) * 512], op=ALU.mult),
                            reads=[tmp.buf, guT.buf], writes=[gm.buf])
                kb.dma("sp", mix_s[:, 8:16, base:base + 2048], gm.ap, gm.sem, reads=[gm.buf])

        def phase_P4(l, last):
            A.reset(PERSIST)
            kb.new_phase()
            xt = T("xt", A.f32([16, 512]), dma=True)
            mixT = T("mixT", A.bf16([16, 512]), dma=True)
            sq = T("sq", A.bf16([16, 512]))
            hT = T("hT", A.bf16([16, 512]))
            aTs = [T(f"aT{i}", A.bf16([16, 512])) for i in range(2)]
            rt = T("rt", A.f32([512]))
            rstd = T("rstd", A.f32([512]))
            rts = [T(f"r{i}", A.f32([512])) for i in range(2)]
            ctr = {"r": 0}
            xsrc = xT_in if l == 0 else xs

            mix_loaded = {}

            def load_mix(t):
                if t < 8 and t not in mix_loaded:
                    mix_loaded[t] = True
                    kb.dma("sp", mixT.ap, mix_s[:, :, t * TT:(t + 1) * TT], mixT.sem, writes=[mixT.buf])

            def tile_fn(t):
                tok0 = t * TT
                load_mix(t)
                kb.dma("sp", xt.ap, xsrc[:, :, tok0:tok0 + TT], xt.sem, writes=[xt.buf])
                for c in range(4):
                    w = next_w(("out", l, c))
                    for mm_ in range(4):
                        m = c * 4 + mm_
                        ps = psum()
                        for kc in range(16):
                            mm(ps.ap, w.ap[:, kc, mm_ * 128:(mm_ + 1) * 128], mixT.ap[:, kc, :], kc == 0, kc == 15,
                               [w.buf, mixT.buf], [ps.buf], kc == 15)
                        kb.op("dve", lambda e, m=m, ps=ps: e.tensor_tensor(out=xt.ap[:, m, :], in0=xt.ap[:, m, :], in1=ps.ap, op=ALU.add),
                              reads=[ps.buf, xt.buf], writes=[xt.buf])
                load_mix(t + 1)
                rms_norm(xt, sq, rt, rstd, gT[:, 2 + l, :], hT.ap, hT.buf)
                for J in range(4):
                    aT = aTs[J % 2]
                    for c in range(4):
                        w = next_w(("up", l, J * 4 + c))
                        for cb in range(4):
                            j = c * 4 + cb
                            ps = psum()
                            for kc in range(16):
                                mm(ps.ap, w.ap[:, kc, cb * 128:(cb + 1) * 128], hT.ap[:, kc, :], kc == 0, kc == 15,
                                   [w.buf, hT.buf], [ps.buf], kc == 15)
                            r = rts[ctr["r"] % 2]
                            ctr["r"] += 1
                            kb.op("act", lambda e, r=r, ps=ps: e.activation(out=r.ap, in_=ps.ap, func=AF.Relu), reads=[ps.buf], writes=[r.buf])
                            kb.op("dve", lambda e, r=r, aT=aT, j=j: e.tensor_tensor(out=aT.ap[:, j, :], in0=r.ap, in1=r.ap, op=ALU.mult),
                                  reads=[r.buf], writes=[aT.buf])
                    for cc in range(4):
                        w = next_w(("down", l, (J, cc)))
                        for mm_ in range(4):
                            m = cc * 4 + mm_
                            ps = psum()
                            for jc in range(16):
                                mm(ps.ap, w.ap[:, jc, mm_ * 128:(mm_ + 1) * 128], aT.ap[:, jc, :], jc == 0, jc == 15,
                                   [w.buf, aT.buf], [ps.buf], jc == 15)
                            kb.op("dve", lambda e, m=m, ps=ps: e.tensor_tensor(out=xt.ap[:, m, :], in0=xt.ap[:, m, :], in1=ps.ap, op=ALU.add),
                                  reads=[ps.buf, xt.buf], writes=[xt.buf])
                if not last:
                    kb.dma("sp", xs[:, :, tok0:tok0 + TT], xt.ap, xt.sem, reads=[xt.buf])
                else:
                    rms_norm(xt, sq, rt, rstd, gT[:, 4, :], xt.ap, xt.buf)
                    kb.dma("sp", yT[:, :, tok0 - 2048:tok0 - 2048 + TT], xt.ap, xt.sem, reads=[xt.buf])
            return tile_fn

        cur = None
        for ph in plan:
            if ph[0] == "P1":
                if cur != ("P1", ph[1]):
                    kb.barrier()
                    fn = phase_P1(ph[1])
                    cur = ("P1", ph[1])
                fn(ph[2], ph[3])
            elif ph[0] == "P2":
                kb.barrier()
                phase_P2(ph[1])
                cur = None
            elif ph[0] == "P3":
                kb.barrier()
                phase_P3(ph[1])
                cur = None
            else:
                if cur != ("P4", ph[1]):
                    kb.barrier()
                    fn = phase_P4(ph[1], ph[1] == nlayers - 1)
                    cur = ("P4", ph[1])
                fn(ph[2])
        assert wst["used"] == len(wsched)
        kb.barrier(final=True)
        stats_nops = kb.nops

        def run(q):
            def body(e):
                for f in q:
                    f(e)
            return body

        block.tensor(run(kb.E["pe"].q))
        block.scalar(run(kb.E["act"].q))
        block.vector(run(kb.E["dve"].q))
        block.gpsimd(run(kb.E["pool"].q))
        block.sync(run(kb.E["sp"].q))
        stats = {n: len(e.q) for n, e in kb.E.items()}
        stats['nops'] = stats_nops
    return nc, stats


def host_consts(hf):
    half = 64
    inv_freq = (10000.0 ** (-np.arange(half, dtype=np.float32) / half)).astype(np.float32)
    pos = np.arange(W) if hf == 1 else (np.arange(W) % 2048)
    ang = pos.astype(np.float32)[None, :] * np.concatenate([inv_freq, inv_freq])[:, None]
    cosT = np.cos(ang).astype(np.float32)
    sinT = np.sin(ang).astype(np.float32)
    ident = np.eye(128, dtype=np.float32)
    rmat = np.zeros((128, 128), np.float32)
    for dp in range(64):
        rmat[dp + 64, dp] = -1.0
        rmat[dp, dp + 64] = 1.0
    k = np.arange(128)[:, None]
    q = np.arange(128)[None, :]
    tril = (k <= q).astype(np.float32)
    cst = np.stack([ident, rmat, tril, np.zeros((128, 128), np.float32)], axis=1)
    mb_cur = np.where(k <= q, 0.0, NEG).astype(np.float32)
    mb_prev = np.where(k >= q, 0.0, NEG).astype(np.float32)
    mb_prevh = mb_prev if hf == 1 else np.full((128, 128), NEG, np.float32)
    mb = np.stack([np.concatenate([mb_prev, mb_cur], 1), np.concatenate([mb_prevh, mb_cur], 1)], axis=1)
    return cosT, sinT, np.ascontiguousarray(cst), np.ascontiguousarray(mb)


_CACHE = {}


def kernel(x, norm1_g, w_in, gmlp_ln_g, gmlp_ln_b, w_spatial, b_spatial, w_out, norm2_g, w_up, w_down, final_g):
    f = np.float32
    x = np.asarray(x, f)
    w_in = np.ascontiguousarray(np.asarray(w_in, f))
    w_out = np.ascontiguousarray(np.asarray(w_out, f))
    w_up = np.ascontiguousarray(np.asarray(w_up, f))
    w_down = np.ascontiguousarray(np.asarray(w_down, f))
    g_all = np.stack([np.asarray(norm1_g, f)[0], np.asarray(norm1_g, f)[1], np.asarray(norm2_g, f)[0],
                      np.asarray(norm2_g, f)[1], np.asarray(final_g, f)], 0)
    gT = np.ascontiguousarray(g_all.reshape(5, 16, 128).transpose(2, 0, 1))
    lng = np.asarray(gmlp_ln_g, f).reshape(2, 1024)
    lnb = np.asarray(gmlp_ln_b, f).reshape(2, 1024)
    ln = np.stack([lng, lnb], 1)
    lnbc = np.ascontiguousarray(np.broadcast_to(ln[None], (128, 2, 2, 1024)))
    bs = np.asarray(b_spatial, f)
    bs4 = np.ascontiguousarray(np.broadcast_to(np.tile(bs, (1, 1, 4))[None], (128, 2, 8, 512)))
    wsT = np.ascontiguousarray(np.asarray(w_spatial, f).transpose(3, 0, 1, 2))
    in_maps = []
    for c in range(8):
        b, hf = c // 2, c % 2
        xw = x[b] if hf == 1 else np.concatenate([x[b, :2048], x[b, :2048]], 0)
        xT = np.ascontiguousarray(xw.reshape(W, 16, 128).transpose(2, 1, 0))
        cosT, sinT, cst, mb = host_consts(hf)
        in_maps.append({"xT": xT, "w_in": w_in, "w_out": w_out, "w_up": w_up, "w_down": w_down, "gT": gT,
                        "lnbc": lnbc, "bs4": bs4, "wsT": wsT, "cosT": cosT, "sinT": sinT, "cst": cst, "mb": mb})
    if "nc" not in _CACHE:
        _CACHE["nc"] = build_program()[0]
    res = run_bass_kernel_spmd(_CACHE["nc"], in_maps, core_ids=list(range(8)))
    out = np.empty((4, SEQ, D), f)
    for c in range(8):
        b, hf = c // 2, c % 2
        yT = np.asarray(res.results[c]["yT"])
        out[b, hf * 2048:(hf + 1) * 2048] = yT.transpose(2, 1, 0).reshape(2048, D)
    return out
```

```python
import math
import numpy as np
import concourse.bass as bass
import concourse.mybir as mybir
from concourse.bass_utils import run_bass_kernel_spmd

F32 = mybir.dt.float32
BF16 = mybir.dt.bfloat16
AF = mybir.ActivationFunctionType
ALU = mybir.AluOpType
AX = mybir.AxisListType

D = 2048
SEQ = 4096
W = 4096
TT = 512
NW = 4
SCALE = 1.0 / math.sqrt(128.0)
NEG = -30000.0


class Buf:
    __slots__ = ("name", "w", "r", "excl")

    def __init__(self, name):
        self.name = name
        self.w = {}
        self.r = {}
        self.excl = False


class DSem:
    def __init__(self, sem):
        self.sem = sem
        self.cnt = 0


class Eng:
    def __init__(self, name, sem):
        self.name = name
        self.sem = sem
        self.cnt = 0
        self.seen = {}
        self.q = []


class Tile:
    def __init__(self, name, ap, sem=None):
        self.buf = Buf(name)
        self.ap = ap
        self.sem = sem


class KB:
    def __init__(self, nc, sem_iter):
        self.nc = nc
        self.sem_iter = sem_iter
        self.E = {n: Eng(n, next(sem_iter)) for n in ("pe", "act", "dve", "pool", "sp")}
        self.dsems = []
        self.pool = []
        self.pool_i = 0
        self.nops = 0
        self.cut = None
        self.n_persist = 0

    def dsem(self):
        if self.pool_i < len(self.pool):
            s = self.pool[self.pool_i]
        else:
            s = DSem(next(self.sem_iter))
            self.dsems.append(s)
            self.pool.append(s)
        self.pool_i += 1
        return s

    def new_phase(self):
        self.pool_i = self.n_persist

    def _wait(self, eng, tok, kind):
        sem, val = tok
        if sem is eng.sem:
            if eng.name == "pe":
                return
        k = id(sem)
        if eng.seen.get(k, 0) >= val:
            return
        eng.seen[k] = val
        eng.q.append(lambda e, s=sem, v=val: e.wait_ge(s, v))

    def _deps(self, eng, reads, writes):
        for b in reads:
            for t in b.w.values():
                self._wait(eng, t, "raw")
            if b.excl:
                for t in b.r.values():
                    if t[0] is not eng.sem:
                        self._wait(eng, t, "rar")
        for b in writes:
            for t in b.w.values():
                self._wait(eng, t, "waw")
            for t in b.r.values():
                self._wait(eng, t, "war")

    def _record(self, tok, reads, writes):
        k = id(tok[0])
        for b in reads:
            b.r[k] = tok
        for b in writes:
            b.w = {k: tok}
            b.r = {}

    def op(self, ename, fn, reads=(), writes=(), signal=True):
        eng = self.E[ename]
        self.nops += 1
        if self.cut is not None and self.nops > self.cut:
            return None
        self._deps(eng, reads, writes)
        if signal:
            eng.cnt += 1
            tok = (eng.sem, eng.cnt)
            eng.q.append(lambda e, f=fn, s=eng.sem: f(e).then_inc(s, 1))
        else:
            tok = (eng.sem, eng.cnt + 1)
            eng.q.append(lambda e, f=fn: f(e))
        self._record(tok, reads, writes)
        return tok

    def dma(self, qname, out, in_, sem, reads=(), writes=()):
        eng = self.E[qname]
        self.nops += 1
        if self.cut is not None and self.nops > self.cut:
            return None
        self._deps(eng, reads, writes)
        sem.cnt += 16
        tok = (sem.sem, sem.cnt)
        eng.q.append(lambda e, o=out, i=in_, s=sem.sem: e.dma_start(out=o, in_=i).then_inc(s, 16))
        self._record(tok, reads, writes)
        return tok

    def barrier(self, engines=("pe", "act", "dve", "pool", "sp"), final=False):
        if self.cut is not None and self.nops > self.cut and not final:
            return
        toks = [(e.sem, e.cnt) for e in self.E.values() if e.cnt > 0]
        toks += [(s.sem, s.cnt) for s in self.dsems if s.cnt > 0 and not getattr(s, "is_w", False)]
        for en in engines:
            eng = self.E[en]
            for t in toks:
                if t[0] is eng.sem:
                    continue
                self._wait(eng, t, "raw")


class Arena:
    def __init__(self, ap_f32, nwords):
        self.ap = ap_f32
        self.n = nwords
        self.top = 0

    def mark(self):
        return self.top

    def reset(self, m):
        self.top = m

    def _take(self, words):
        words = (words + 7) // 8 * 8
        o = self.top
        self.top += words
        assert self.top <= self.n, f"arena overflow {self.top} > {self.n}"
        return o

    def f32(self, shape):
        n = int(np.prod(shape))
        o = self._take(n)
        ap = self.ap[:, o:o + n]
        return self._shape(ap, shape)

    def bf16(self, shape):
        n = int(np.prod(shape))
        o = self._take((n + 1) // 2)
        ap = self.ap[:, o:o + (n + 1) // 2].bitcast(BF16)[:, 0:n]
        return self._shape(ap, shape)

    @staticmethod
    def _shape(ap, shape):
        if len(shape) == 1:
            return ap
        if len(shape) == 2:
            return ap.rearrange("p (a b) -> p a b", b=shape[1])
        if len(shape) == 3:
            return ap.rearrange("p (a b c) -> p a b c", b=shape[1], c=shape[2])
        raise ValueError(shape)


def build_program(nlayers=2, dbg=False, upto=None, cut=None):
    nc = bass.Bass("TRN2", target_bir_lowering=False)

    def din(name, shape, dt=F32):
        return nc.dram_tensor(name, list(shape), dt, kind="ExternalInput")

    xT_in = din("xT", [128, 16, W]).ap()
    w_in = din("w_in", [2, D, 5120])
    w_out = din("w_out", [2, D, D])
    w_up = din("w_up", [2, D, 4 * D])
    w_down = din("w_down", [2, 4 * D, D])
    gT_in = din("gT", [128, 5, 16]).ap()
    ln_in = din("lnbc", [128, 2, 2, 1024]).ap()
    bs_in = din("bs4", [128, 2, 8, 512]).ap()
    ws_in = din("wsT", [128, 2, 8, 128]).ap()
    cos_in = din("cosT", [128, W]).ap()
    sin_in = din("sinT", [128, W]).ap()
    cst_in = din("cst", [128, 4, 128]).ap()
    mb_in = din("mb", [128, 2, 256]).ap()
    yT = nc.dram_tensor("yT", [128, 16, 2048], F32, kind="ExternalOutput").ap()

    skind = "ExternalOutput" if dbg else "Internal"
    xs = nc.dram_tensor("xs", [128, 16, W], F32, kind=skind).ap()
    qT_s = nc.dram_tensor("qT_s", [128, 8, W], BF16, kind=skind).ap()
    kT_s = nc.dram_tensor("kT_s", [128, 8, W], BF16, kind=skind).ap()
    v_s = nc.dram_tensor("v_s", [8, W, 128], BF16, kind=skind).ap()
    gu_s = nc.dram_tensor("gu_s", [128, 8, W], BF16, kind=skind).ap()
    gv_s = nc.dram_tensor("gv_s", [W, 1024], BF16, kind=skind).ap()
    mix_s = nc.dram_tensor("mix_s", [128, 16, W], BF16, kind=skind).ap()

    NWORDS = 53200
    NSEM = 40
    import contextlib
    with contextlib.ExitStack() as es:
        arena_t = es.enter_context(nc.sbuf_tensor("arena", [128, NWORDS], F32))
        banks = [es.enter_context(nc.psum_tensor(f"ps{i}", [128, 512], F32)) for i in range(8)]
        sems = [es.enter_context(nc.semaphore(f"s{i}")) for i in range(NSEM)]
        block = es.enter_context(nc.Block())
        kb = KB(nc, iter(sems))
        kb.cut = cut
        A = Arena(arena_t[:, :], NWORDS)

        def T(name, ap, dma=False):
            return Tile(name, ap, kb.dsem() if dma else None)

        ps_tiles = [Tile(f"ps{i}", banks[i][:, :]) for i in range(8)]
        for t_ in ps_tiles:
            t_.buf.excl = True
        ps_i = [0]

        def psum():
            t = ps_tiles[ps_i[0] % 8]
            ps_i[0] += 1
            return t

        wslots = [T(f"w{i}", A.bf16([16, 512]), dma=True) for i in range(NW)]
        for s in wslots:
            s.sem.is_w = True
        ones_bf = A.bf16([128])
        ident_bf = A.bf16([128])
        rmat_bf = A.bf16([128])
        tril_f = A.f32([128])
        mb_bf = A.bf16([2, 256])
        gT = A.f32([5, 16])
        eps6 = A.f32([1])
        eps5 = A.f32([1])
        cst_f = T("cst_f", A.f32([4, 128]), dma=True)
        mb_f = T("mb_f", A.f32([2, 256]), dma=True)
        gT_t = T("gT", gT, dma=True)
        PERSIST = A.mark()
        kb.n_persist = kb.pool_i

        kb.dma("sp", cst_f.ap, cst_in, cst_f.sem, writes=[cst_f.buf])
        kb.dma("sp", mb_f.ap, mb_in, mb_f.sem, writes=[mb_f.buf])
        kb.dma("sp", gT_t.ap, gT_in, gT_t.sem, writes=[gT_t.buf])
        cbuf = Buf("consts")
        kb.op("dve", lambda e: e.memset(ones_bf, 1.0), writes=[cbuf])
        kb.op("dve", lambda e: e.memset(eps6, 1e-6), writes=[cbuf])
        kb.op("dve", lambda e: e.memset(eps5, 1e-5), writes=[cbuf])
        kb.op("dve", lambda e: e.tensor_copy(out=ident_bf, in_=cst_f.ap[:, 0, :]), reads=[cst_f.buf], writes=[cbuf])
        kb.op("dve", lambda e: e.tensor_copy(out=rmat_bf, in_=cst_f.ap[:, 1, :]), reads=[cst_f.buf], writes=[cbuf])
        kb.op("dve", lambda e: e.tensor_copy(out=tril_f, in_=cst_f.ap[:, 2, :]), reads=[cst_f.buf], writes=[cbuf])
        kb.op("dve", lambda e: e.tensor_copy(out=mb_bf, in_=mb_f.ap), reads=[mb_f.buf], writes=[cbuf])
        kb.barrier()

        def wsrc(kind, l, c):
            if kind == "in":
                return w_in[l].rearrange("(kc p) n -> p kc n", p=128)[:, :, c * 512:(c + 1) * 512]
            if kind == "out":
                return w_out[l].rearrange("(kc p) n -> p kc n", p=128)[:, :, c * 512:(c + 1) * 512]
            if kind == "up":
                return w_up[l].rearrange("(kc p) n -> p kc n", p=128)[:, :, c * 512:(c + 1) * 512]
            J, cc = c
            return w_down[l][J * 2048:(J + 1) * 2048].rearrange("(jc p) n -> p jc n", p=128)[:, :, cc * 512:(cc + 1) * 512]

        def p1_chunks(full):
            return list(range(10)) if full else [2, 3, 4, 5]

        plan = []
        for l in range(nlayers):
            for t in range(8):
                plan.append(("P1", l, t, (l == 0) or (t >= 4)))
            plan.append(("P2", l))
            plan.append(("P3", l))
            for t in (range(8) if (l == 0 and nlayers > 1) else range(4, 8)):
                plan.append(("P4", l, t))
        if upto is not None:
            plan = plan[:upto]
        wsched = []
        for ph in plan:
            if ph[0] == "P1":
                wsched += [("in", ph[1], c) for c in p1_chunks(ph[3])]
            elif ph[0] == "P4":
                l = ph[1]
                wsched += [("out", l, c) for c in range(4)]
                for J in range(4):
                    wsched += [("up", l, J * 4 + c) for c in range(4)]
                    wsched += [("down", l, (J, cc)) for cc in range(4)]
        wst = {"issued": 0, "used": 0}

        def next_w(expect):
            i = wst["used"]
            assert wsched[i] == expect, (wsched[i], expect)
            wst["used"] += 1
            while wst["issued"] < min(len(wsched), i + NW):
                j = wst["issued"]
                slot = wslots[j % NW]
                kb.dma("pool", slot.ap, wsrc(*wsched[j]), slot.sem, writes=[slot.buf])
                wst["issued"] += 1
            return wslots[i % NW]

        def mm(out, lhsT, rhs, start, stop, reads, writes, signal):
            kb.op("pe", lambda e: e.matmul(out, lhsT=lhsT, rhs=rhs, start=start, stop=stop),
                  reads=reads, writes=writes, signal=signal)

        def rms_norm(xt, sq, rt, rstd, g_ap, out_ap, out_buf):
            kb.op("act", lambda e: e.activation(out=sq.ap, in_=xt.ap, func=AF.Square), reads=[xt.buf], writes=[sq.buf])
            ps = psum()
            for fc in range(16):
                mm(ps.ap, ones_bf, sq.ap[:, fc, :], fc == 0, fc == 15, [sq.buf], [ps.buf], fc == 15)
            kb.op("act", lambda e: e.activation(out=rt.ap, in_=ps.ap, func=AF.Sqrt, bias=eps6, scale=1.0 / D),
                  reads=[ps.buf], writes=[rt.buf])
            kb.op("dve", lambda e: e.reciprocal(out=rstd.ap, in_=rt.ap), reads=[rt.buf], writes=[rstd.buf])
            inplace = out_buf is xt.buf
            for fc in range(16):
                edge = fc in (0, 15)
                kb.op("dve", lambda e, fc=fc: e.scalar_tensor_tensor(
                    out=out_ap[:, fc, :], in0=xt.ap[:, fc, :], scalar=g_ap[:, fc:fc + 1], in1=rstd.ap,
                    op0=ALU.mult, op1=ALU.mult), reads=([] if (inplace and not edge) else [xt.buf]) + [rstd.buf],
                    writes=[out_buf] if edge else [])

        def gelu(ps, x2s, gu_, gs, out_ap, out_buf):
            kb.op("act", lambda e: e.activation(out=x2s.ap, in_=ps.ap, func=AF.Square, scale=math.sqrt(0.044715)),
                  reads=[ps.buf], writes=[x2s.buf])
            kb.op("dve", lambda e: e.scalar_tensor_tensor(out=gu_.ap, in0=x2s.ap, scalar=1.0, in1=ps.ap, op0=ALU.add, op1=ALU.mult),
                  reads=[x2s.buf, ps.buf], writes=[gu_.buf])
            kb.op("act", lambda e: e.activation(out=gs.ap, in_=gu_.ap, func=AF.Sigmoid, scale=1.5957691216057308),
                  reads=[gu_.buf], writes=[gs.buf])
            kb.op("dve", lambda e: e.tensor_tensor(out=out_ap, in0=gs.ap, in1=ps.ap, op=ALU.mult), reads=[gs.buf, ps.buf], writes=[out_buf])

        def phase_P1(l):
            A.reset(PERSIST)
            kb.new_phase()
            xt = T("xt", A.f32([16, 512]), dma=True)
            sq = T("sq", A.bf16([16, 512]))
            hTs = [T(f"hT{i}", A.bf16([16, 512])) for i in range(2)]
            cos = T("cos", A.f32([512]), dma=True)
            sin = T("sin", A.f32([512]), dma=True)
            lnp = T("lnp", A.f32([2, 1024]), dma=True)
            rt = T("rt", A.f32([512]))
            rstd = T("rstd", A.f32([512]))
            zb = [T(f"zb{i}", A.bf16([512])) for i in range(2)]
            t1s = [T(f"t1{i}", A.f32([512])) for i in range(3)]
            t2s = [T(f"t2{i}", A.f32([512])) for i in range(3)]
            qk_st = [T(f"qkst{i}", A.bf16([4, 512]), dma=True) for i in range(2)]
            v_st = [T(f"vst{i}", A.bf16([512]), dma=True) for i in range(2)]
            gv_st = [T(f"gvst{i}", A.bf16([512]), dma=True) for i in range(2)]
            NSET = 5
            gvws = [T(f"gvw{i}", A.f32([512])) for i in range(NSET)]
            sqvs = [T(f"sqv{i}", A.f32([512])) for i in range(2)]
            st4s = [[T(f"st4{i}{j}", A.f32([4])) for i in range(5)] for j in range(NSET)]
            pendB = []
            ctr = {"zb": 0, "st": 0, "v": 0, "gv": 0, "g": 0, "t": 0, "w": 0}
            kb.dma("sp", lnp.ap, ln_in[:, l, :, :], lnp.sem, writes=[lnp.buf])
            xsrc = xT_in if l == 0 else xs
            pend = []

            post = []

            def run_pending(keep=0):
                posts_now = post[:]
                del post[:]
                while len(pend) > keep:
                    r = pend.pop(0)()
                    if r is not None:
                        post.append(r)
                for p_ in posts_now:
                    p_()

            xt_loaded = {}

            def load_xt(t):
                if t < 8 and t not in xt_loaded:
                    xt_loaded[t] = True
                    kb.dma("sp", xt.ap, xsrc[:, :, t * TT:(t + 1) * TT], xt.sem, writes=[xt.buf])

            prepped = {}

            def prep(t):
                if t >= 8 or t in prepped:
                    return
                prepped[t] = True
                load_xt(t)
                h_ = hTs[t % 2]
                rms_norm(xt, sq, rt, rstd, gT[:, l, :], h_.ap, h_.buf)

            def tile_fn(t, full):
                tok0 = t * TT
                prep(t)
                load_xt(t + 1)
                hT = hTs[t % 2]
                kb.dma("sp", cos.ap, cos_in[:, tok0:tok0 + TT], cos.sem, writes=[cos.buf])
                kb.dma("sp", sin.ap, sin_in[:, tok0:tok0 + TT], sin.sem, writes=[sin.buf])
                chunks = p1_chunks(full)
                for ci, c in enumerate(chunks):
                    if ci == len(chunks) // 2:
                        prep(t + 1)
                    w = next_w(("in", l, c))
                    if c in (0, 1, 2, 3, 6, 7):
                        st = qk_st[ctr["st"] % 2]
                        ctr["st"] += 1
                        for cb in range(4):
                            ps = psum()
                            for kc in range(16):
                                mm(ps.ap, w.ap[:, kc, cb * 128:(cb + 1) * 128], hT.ap[:, kc, :], kc == 0, kc == 15,
                                   [w.buf, hT.buf], [ps.buf], kc == 15)
                            run_pending()
                            if c < 4:
                                def evac(ps=ps, st=st, cb=cb):
                                    z = zb[ctr["zb"] % 2]
                                    ctr["zb"] += 1
                                    t1 = t1s[ctr["t"] % 3]
                                    t2 = t2s[ctr["t"] % 3]
                                    ctr["t"] += 1
                                    kb.op("act", lambda e: e.activation(out=z.ap, in_=ps.ap, func=AF.Copy),
                                          reads=[ps.buf], writes=[z.buf])
                                    ps2 = psum()
                                    mm(ps2.ap, rmat_bf, z.ap, True, True, [z.buf], [ps2.buf], True)
                                    kb.op("dve", lambda e: e.tensor_tensor(out=t1.ap, in0=ps.ap, in1=cos.ap, op=ALU.mult),
                                          reads=[ps.buf, cos.buf], writes=[t1.buf])
                                    kb.op("dve", lambda e: e.tensor_tensor(out=t2.ap, in0=ps2.ap, in1=sin.ap, op=ALU.mult),
                                          reads=[ps2.buf, sin.buf], writes=[t2.buf])
                                    def add_(t1=t1, t2=t2):
                                        kb.op("dve", lambda e: e.tensor_tensor(out=st.ap[:, cb, :], in0=t1.ap, in1=t2.ap, op=ALU.add),
                                              reads=[t1.buf, t2.buf], writes=[st.buf])
                                    return add_
                            else:
                                def evac(ps=ps, st=st, cb=cb):
                                    kb.op("act", lambda e: e.activation(out=st.ap[:, cb, :], in_=ps.ap, func=AF.Gelu_apprx_tanh),
                                          reads=[ps.buf], writes=[st.buf])
                            if cb == 3:
                                dst = {0: qT_s, 1: qT_s, 2: kT_s, 3: kT_s, 6: gu_s, 7: gu_s}[c]
                                h0 = (c % 2) * 4

                                def evac_last(evac=evac, st=st, dst=dst, h0=h0):
                                    r = evac()

                                    def fin():
                                        if r is not None:
                                            r()
                                        kb.dma("sp", dst[:, h0:h0 + 4, tok0:tok0 + TT], st.ap, st.sem, reads=[st.buf])
                                    return fin
                                pend.append(evac_last)
                            else:
                                pend.append(evac)
                    else:
                        for tb in range(4):
                            ps = psum()
                            for kc in range(16):
                                mm(ps.ap, hT.ap[:, kc, tb * 128:(tb + 1) * 128], w.ap[:, kc, :], kc == 0, kc == 15,
                                   [w.buf, hT.buf], [ps.buf], kc == 15)
                            run_pending()
                            r0 = tok0 + tb * 128
                            if c in (4, 5):
                                def evac(ps=ps, r0=r0, c=c):
                                    vs = v_st[ctr["v"] % 2]
                                    ctr["v"] += 1
                                    kb.op("act", lambda e: e.activation(out=vs.ap, in_=ps.ap, func=AF.Copy),
                                          reads=[ps.buf], writes=[vs.buf])
                                    h0 = (c - 4) * 4
                                    kb.dma("sp", v_s[h0:h0 + 4, r0:r0 + 128, :].rearrange("h t d -> t h d"),
                                           vs.ap.rearrange("p (h d) -> p h d", d=128), vs.sem, reads=[vs.buf])
                            else:
                                def evac(ps=ps, r0=r0, c=c, tb=tb):
                                    j = ctr["w"] % NSET
                                    ctr["w"] += 1
                                    gw = gvws[j]
                                    sqv = sqvs[j % 2]
                                    sm, mean, vsum, sd, rs = st4s[j]
                                    kb.op("act", lambda e: e.activation(out=gw.ap, in_=ps.ap, func=AF.Gelu_apprx_tanh),
                                          reads=[ps.buf], writes=[gw.buf])
                                    kb.op("act", lambda e: e.activation(out=sqv.ap, in_=gw.ap, func=AF.Square), reads=[gw.buf], writes=[sqv.buf])
                                    x3 = gw.ap.rearrange("p (g d) -> p g d", d=128)
                                    s3 = sqv.ap.rearrange("p (g d) -> p g d", d=128)
                                    kb.op("dve", lambda e: e.tensor_reduce(out=sm.ap, in_=x3, axis=AX.X, op=ALU.add),
                                          reads=[gw.buf], writes=[sm.buf])
                                    kb.op("dve", lambda e: e.tensor_reduce(out=vsum.ap, in_=s3, axis=AX.X, op=ALU.add),
                                          reads=[sqv.buf], writes=[vsum.buf])
                                    kb.op("dve", lambda e: e.tensor_scalar(out=mean.ap, in0=sm.ap, scalar1=1.0 / 128, scalar2=None, op0=ALU.mult),
                                          reads=[sm.buf], writes=[mean.buf])
                                    kb.op("dve", lambda e: e.tensor_tensor(out=sm.ap, in0=mean.ap, in1=mean.ap, op=ALU.mult),
                                          reads=[mean.buf], writes=[sm.buf])
                                    kb.op("dve", lambda e: e.scalar_tensor_tensor(out=vsum.ap, in0=vsum.ap, scalar=1.0 / 128, in1=sm.ap,
                                                                                  op0=ALU.mult, op1=ALU.subtract),
                                          reads=[vsum.buf, sm.buf], writes=[vsum.buf])

                                    def b1():
                                        kb.op("act", lambda e: e.activation(out=sd.ap, in_=vsum.ap, func=AF.Sqrt, bias=eps5, scale=1.0),
                                              reads=[vsum.buf], writes=[sd.buf])

                                    def b2():
                                        kb.op("dve", lambda e: e.reciprocal(out=rs.ap, in_=sd.ap), reads=[sd.buf], writes=[rs.buf])
                                        kb.op("dve", lambda e: e.scalar_tensor_tensor(out=sm.ap, in0=mean.ap, scalar=-1.0, in1=rs.ap,
                                                                                      op0=ALU.mult, op1=ALU.mult),
                                              reads=[mean.buf, rs.buf], writes=[sm.buf])
                                        c0 = (c - 8) * 512
                                        for gg in range(4):
                                            edge = gg in (0, 3)
                                            kb.op("dve", lambda e, gg=gg: e.tensor_scalar(
                                                out=gw.ap[:, gg * 128:(gg + 1) * 128], in0=gw.ap[:, gg * 128:(gg + 1) * 128],
                                                scalar1=rs.ap[:, gg:gg + 1], scalar2=sm.ap[:, gg:gg + 1], op0=ALU.mult, op1=ALU.add),
                                                reads=[rs.buf, sm.buf] + ([gw.buf] if edge else []), writes=[gw.buf] if edge else [])
                                        kb.op("dve", lambda e: e.tensor_tensor(out=gw.ap, in0=gw.ap, in1=lnp.ap[:, 0, c0:c0 + 512], op=ALU.mult),
                                              reads=[gw.buf, lnp.buf], writes=[gw.buf])
                                        gst = gv_st[ctr["gv"] % 2]
                                        ctr["gv"] += 1
                                        kb.op("dve", lambda e: e.tensor_tensor(out=gst.ap, in0=gw.ap, in1=lnp.ap[:, 1, c0:c0 + 512], op=ALU.add),
                                              reads=[gw.buf, lnp.buf], writes=[gst.buf])
                                        kb.dma("sp", gv_s[r0:r0 + 128, c0:c0 + 512], gst.ap, gst.sem, reads=[gst.buf])
                                    pendB.append((b1, b2))
                                    if tb == 3:
                                        def flushB():
                                            bs_ = pendB[:]
                                            del pendB[:]
                                            for x1_, _ in bs_:
                                                x1_()
                                            for _, x2_ in bs_:
                                                x2_()
                                        return flushB
                            pend.append(evac)
                run_pending()
                run_pending()
            return tile_fn

        def phase_P2(l):
            A.reset(PERSIST)
            kb.new_phase()
            halves = [0, 1] if l == 0 else [1]
            hb = []
            for i in range(2):
                hb.append(dict(
                    QT=T(f"QT{i}", A.bf16([W]), dma=True), KT=T(f"KT{i}", A.bf16([W]), dma=True),
                    V1=T(f"V1{i}", A.bf16([32, 128]), dma=True), V2=T(f"V2{i}", A.bf16([32, 128]), dma=True),
                    V3=T(f"V3{i}", A.bf16([32, 128]), dma=True)))
            accs = [T(f"acc{i}", A.f32([2, 2048])) for i in range(2)]
            rls = [T("rl0", A.f32([2048]))] * 2
            pipe = []
            LAG = 4
            pTs = [T(f"pT{i}", A.bf16([256])) for i in range(7)]
            mos = [T(f"mo{i}", A.bf16([2048]), dma=True) for i in range(2)]
            ctr = {"pT": 0, "mo": 0}

            def load(h):
                b = hb[h % 2]
                kb.dma("sp", b["QT"].ap, qT_s[:, h, :], b["QT"].sem, writes=[b["QT"].buf])
                kb.dma("sp", b["KT"].ap, kT_s[:, h, :], b["KT"].sem, writes=[b["KT"].buf])
                kb.dma("sp", b["V1"].ap, v_s[h].rearrange("(b j) d -> j b d", j=128), b["V1"].sem, writes=[b["V1"].buf])
                kb.dma("sp", b["V2"].ap.rearrange("p (m r) d -> p m r d", r=4),
                       v_s[h].rearrange("(m j r) d -> j m r d", j=128, r=4), b["V2"].sem, writes=[b["V2"].buf])
                kb.dma("sp", b["V3"].ap.rearrange("p (m r) d -> p m r d", r=16),
                       v_s[h].rearrange("(m j r) d -> j m r d", j=128, r=16), b["V3"].sem, writes=[b["V3"].buf])

            load(0)
            for h in range(8):
                b = hb[h % 2]
                QT, KT = b["QT"], b["KT"]
                for H in halves:
                    base = H * 2048
                    acc = accs[ctr["mo"] % 2]
                    rl = rls[ctr["mo"] % 2]
                    blocks = []
                    for n in range(16):
                        s0 = base + n * 128
                        prev = None
                        if not (H == 0 and n == 0):
                            prev = (KT.ap[:, s0 - 128:s0], b["V1"], s0 // 128 - 1, 1 if n == 0 else 0)
                        blocks.append((True, QT.ap[:, s0:s0 + 128], acc.ap[:, :, n * 128:(n + 1) * 128],
                                       (KT.ap[:, s0:s0 + 128], b["V1"], s0 // 128), prev))
                    for m in range(4):
                        for r in range(4):
                            s0 = base + m * 512
                            prev = None
                            if not (H == 0 and m == 0):
                                prev = (KT.ap[:, s0 - 512 + r:s0:4], b["V2"], (s0 // 512 - 1) * 4 + r, 1 if m == 0 else 0)
                            blocks.append((False, QT.ap[:, s0 + r:s0 + 512:4], acc.ap[:, :, m * 512 + r:(m + 1) * 512:4],
                                           (KT.ap[:, s0 + r:s0 + 512:4], b["V2"], (s0 // 512) * 4 + r), prev))
                    for r in range(16):
                        prev = None
                        if H == 1:
                            prev = (KT.ap[:, r:2048:16], b["V3"], r, 1)
                        blocks.append((False, QT.ap[:, base + r:base + 2048:16], acc.ap[:, :, r:2048:16],
                                       (KT.ap[:, base + r:base + 2048:16], b["V3"], H * 16 + r), prev))
                    def stage1(blk):
                        (first, q_ap, acc_v, cur, prev) = blk
                        sc = psum()
                        rd = [QT.buf, KT.buf]
                        if prev is not None:
                            mbp = mb_bf[:, prev[3], 0:128]
                            mm(sc.ap[:, 0:128], ident_bf, mbp, True, False, [], [sc.buf], False)
                            mm(sc.ap[:, 0:128], prev[0], q_ap, False, True, rd, [sc.buf], False)
                        mm(sc.ap[:, 128:256], ident_bf, mb_bf[:, 0, 128:256], True, False, [], [sc.buf], False)
                        mm(sc.ap[:, 128:256], cur[0], q_ap, False, True, rd, [sc.buf], True)
                        lo = 0 if prev is not None else 128
                        pT = pTs[ctr["pT"] % len(pTs)]
                        ctr["pT"] += 1
                        kb.op("act", lambda e, pT=pT, sc=sc, lo=lo: e.activation(out=pT.ap[:, lo:256], in_=sc.ap[:, lo:256], func=AF.Exp, scale=SCALE),
                              reads=[sc.buf], writes=[pT.buf])
                        return pT

                    def stage2(blk, pT, acc=acc):
                        (first, q_ap, acc_v, cur, prev) = blk
                        ol = psum()
                        vc = cur[1].ap[:, cur[2], :]
                        if prev is not None:
                            vp = prev[1].ap[:, prev[2], :]
                            mm(ol.ap[:, 0:128], vp, pT.ap[:, 0:128], True, False, [pT.buf, prev[1].buf], [ol.buf], False)
                            mm(ol.ap[:, 0:128], vc, pT.ap[:, 128:256], False, True, [pT.buf, cur[1].buf], [ol.buf], False)
                            mm(ol.ap[:, 128:256], ones_bf, pT.ap[:, 0:128], True, False, [pT.buf], [ol.buf], False)
                            mm(ol.ap[:, 128:256], ones_bf, pT.ap[:, 128:256], False, True, [pT.buf], [ol.buf], True)
                        else:
                            mm(ol.ap[:, 0:128], vc, pT.ap[:, 128:256], True, True, [pT.buf, cur[1].buf], [ol.buf], False)
                            mm(ol.ap[:, 128:256], ones_bf, pT.ap[:, 128:256], True, True, [pT.buf], [ol.buf], True)
                        olv = ol.ap[:, 0:256].rearrange("p (c q) -> p c q", c=2)
                        if first:
                            kb.op("act", lambda e, acc_v=acc_v, olv=olv: e.activation(out=acc_v, in_=olv, func=AF.Copy),
                                  reads=[ol.buf], writes=[acc.buf])
                        else:
                            kb.op("dve", lambda e, acc_v=acc_v, olv=olv: e.tensor_tensor(out=acc_v, in0=acc_v, in1=olv, op=ALU.add),
                                  reads=[ol.buf, acc.buf], writes=[acc.buf])

                    mo = mos[ctr["mo"] % 2]
                    ctr["mo"] += 1

                    def fin(acc=acc, rl=rl, mo=mo, h=h, base=base):
                        kb.op("act", lambda e: e.activation(out=rl.ap, in_=acc.ap[:, 1, :], func=AF.Ln), reads=[acc.buf], writes=[rl.buf])
                        kb.op("act", lambda e: e.activation(out=rl.ap, in_=rl.ap, func=AF.Exp, scale=-1.0), reads=[rl.buf], writes=[rl.buf])
                        kb.op("dve", lambda e: e.tensor_tensor(out=mo.ap, in0=acc.ap[:, 0, :], in1=rl.ap, op=ALU.mult),
                              reads=[acc.buf, rl.buf], writes=[mo.buf])
                        kb.dma("sp", mix_s[:, h, base:base + 2048], mo.ap, mo.sem, reads=[mo.buf])

                    for bi, blk in enumerate(blocks):
                        pipe.append((stage2, blk, stage1(blk), fin if bi == len(blocks) - 1 else None))
                        if len(pipe) > LAG:
                            s2, b_, p_, f_ = pipe.pop(0)
                            s2(b_, p_)
                            if f_ is not None:
                                f_()
                        if bi == LAG + 1 and H == halves[0] and h + 1 < 8:
                            load(h + 1)
            while pipe:
                s2, b_, p_, f_ = pipe.pop(0)
                s2(b_, p_)
                if f_ is not None:
                    f_()

        def phase_P3(l):
            A.reset(PERSIST)
            kb.new_phase()
            halves = [0, 1] if l == 0 else [1]
            ws_f = T("ws_f", A.f32([8, 128]), dma=True)
            wm = T("wm", A.bf16([8, 128]))
            bs4 = T("bs4", A.f32([8, 512]), dma=True)
            gvt = T("gvt", A.bf16([16, 1024]), dma=True)
            guT = T("guT", A.bf16([8, 2048]), dma=True)
            gm = T("gm", A.bf16([8, 2048]), dma=True)
            tmps = [T(f"tmp{i}", A.f32([512])) for i in range(2)]
            kb.dma("sp", ws_f.ap, ws_in[:, l, :, :], ws_f.sem, writes=[ws_f.buf])
            kb.dma("sp", bs4.ap, bs_in[:, l, :, :], bs4.sem, writes=[bs4.buf])
            for g in range(8):
                kb.op("dve", lambda e, g=g: e.tensor_tensor(out=wm.ap[:, g, :], in0=ws_f.ap[:, g, :], in1=tril_f, op=ALU.mult),
                      reads=[ws_f.buf], writes=[wm.buf])
            k = 0
            for H in halves:
                base = H * 2048
                kb.dma("sp", gvt.ap, gv_s[base:base + 2048, :].rearrange("(c s) f -> s c f", s=128), gvt.sem, writes=[gvt.buf])
                kb.dma("sp", guT.ap, gu_s[:, :, base:base + 2048], guT.sem, writes=[guT.buf])
                for g in range(8):
                    for c4 in range(4):
                        ps = psum()
                        for cc in range(4):
                            c = c4 * 4 + cc
                            mm(ps.ap[:, cc * 128:(cc + 1) * 128], gvt.ap[:, c, g * 128:(g + 1) * 128], wm.ap[:, g, :], True, True,
                               [gvt.buf, wm.buf], [ps.buf], cc == 3)
                        tmp = tmps[k % 2]
                        k += 1
                        kb.op("dve", lambda e, tmp=tmp, ps=ps, g=g: e.tensor_tensor(out=tmp.ap, in0=ps.ap, in1=bs4.ap[:, g, :], op=ALU.add),
                              reads=[ps.buf, bs4.buf], writes=[tmp.buf])
                        kb.op("dve", lambda e, tmp=tmp, g=g, c4=c4: e.tensor_tensor(
                            out=gm.ap[:, g, c4 * 512:(c4 + 1) * 512], in0=tmp.ap, in1=guT.ap[:, g, c4 * 512:(c4 + 1) * 512], op=ALU.mult),
                            reads=[tmp.buf, guT.buf], writes=[gm.buf])
                kb.dma("sp", mix_s[:, 8:16, base:base + 2048], gm.ap, gm.sem, reads=[gm.buf])

        def phase_P4(l, last):
            A.reset(PERSIST)
            kb.new_phase()
            xts = [T(f"xt{i}", A.f32([16, 512]), dma=True) for i in range(2)]
            mixT = T("mixT", A.bf16([16, 512]), dma=True)
            hT = T("hT", A.bf16([16, 512]))
            aTs = [T(f"aT{i}", A.bf16([16, 512])) for i in range(2)]
            sq = aTs[1]
            rt = T("rt", A.f32([512]))
            rstd = T("rstd", A.f32([512]))
            rts = [T(f"r{i}", A.f32([512])) for i in range(2)]
            ctr = {"r": 0}
            xsrc = xT_in if l == 0 else xs

            mix_loaded = {}

            def load_mix(t):
                if t < 8 and t not in mix_loaded:
                    mix_loaded[t] = True
                    kb.dma("sp", mixT.ap, mix_s[:, :, t * TT:(t + 1) * TT], mixT.sem, writes=[mixT.buf])

            x_loaded = {}

            def load_x(t):
                if t < 8 and t not in x_loaded:
                    x_loaded[t] = True
                    x_ = xts[t % 2]
                    kb.dma("sp", x_.ap, xsrc[:, :, t * TT:(t + 1) * TT], x_.sem, writes=[x_.buf])

            def tile_fn(t):
                tok0 = t * TT
                load_mix(t)
                load_x(t)
                xt = xts[t % 2]
                for c in range(4):
                    w = next_w(("out", l, c))
                    for mm_ in range(4):
                        m = c * 4 + mm_
                        ps = psum()
                        for kc in range(16):
                            mm(ps.ap, w.ap[:, kc, mm_ * 128:(mm_ + 1) * 128], mixT.ap[:, kc, :], kc == 0, kc == 15,
                               [w.buf, mixT.buf], [ps.buf], kc == 15)
                        kb.op("dve", lambda e, m=m, ps=ps: e.tensor_tensor(out=xt.ap[:, m, :], in0=xt.ap[:, m, :], in1=ps.ap, op=ALU.add),
                              reads=[ps.buf, xt.buf], writes=[xt.buf])
                load_mix(t + 1)
                load_x(t + 1)
                rms_norm(xt, sq, rt, rstd, gT[:, 2 + l, :], hT.ap, hT.buf)
                for J in range(4):
                    aT = aTs[J % 2]
                    for c in range(4):
                        w = next_w(("up", l, J * 4 + c))
                        for cb in range(4):
                            j = c * 4 + cb
                            ps = psum()
                            for kc in range(16):
                                mm(ps.ap, w.ap[:, kc, cb * 128:(cb + 1) * 128], hT.ap[:, kc, :], kc == 0, kc == 15,
                                   [w.buf, hT.buf], [ps.buf], kc == 15)
                            r = rts[ctr["r"] % 2]
                            ctr["r"] += 1
                            kb.op("act", lambda e, r=r, ps=ps: e.activation(out=r.ap, in_=ps.ap, func=AF.Relu), reads=[ps.buf], writes=[r.buf])
                            kb.op("dve", lambda e, r=r, aT=aT, j=j: e.tensor_tensor(out=aT.ap[:, j, :], in0=r.ap, in1=r.ap, op=ALU.mult),
                                  reads=[r.buf], writes=[aT.buf])
                    for cc in range(4):
                        w = next_w(("down", l, (J, cc)))
                        for mm_ in range(4):
                            m = cc * 4 + mm_
                            ps = psum()
                            for jc in range(16):
                                mm(ps.ap, w.ap[:, jc, mm_ * 128:(mm_ + 1) * 128], aT.ap[:, jc, :], jc == 0, jc == 15,
                                   [w.buf, aT.buf], [ps.buf], jc == 15)
                            kb.op("dve", lambda e, m=m, ps=ps: e.tensor_tensor(out=xt.ap[:, m, :], in0=xt.ap[:, m, :], in1=ps.ap, op=ALU.add),
                                  reads=[ps.buf, xt.buf], writes=[xt.buf])
                if not last:
                    kb.dma("sp", xs[:, :, tok0:tok0 + TT], xt.ap, xt.sem, reads=[xt.buf])
                else:
                    rms_norm(xt, sq, rt, rstd, gT[:, 4, :], xt.ap, xt.buf)
                    kb.dma("sp", yT[:, :, tok0 - 2048:tok0 - 2048 + TT], xt.ap, xt.sem, reads=[xt.buf])
            return tile_fn

        cur = None
        for ph in plan:
            if ph[0] == "P1":
                if cur != ("P1", ph[1]):
                    kb.barrier()
                    fn = phase_P1(ph[1])
                    cur = ("P1", ph[1])
                fn(ph[2], ph[3])
            elif ph[0] == "P2":
                kb.barrier()
                phase_P2(ph[1])
                cur = None
            elif ph[0] == "P3":
                kb.barrier()
                phase_P3(ph[1])
                cur = None
            else:
                if cur != ("P4", ph[1]):
                    kb.barrier()
                    fn = phase_P4(ph[1], ph[1] == nlayers - 1)
                    cur = ("P4", ph[1])
                fn(ph[2])
        assert wst["used"] == len(wsched)
        kb.barrier(final=True)
        stats_nops = kb.nops

        def run(q):
            def body(e):
                for f in q:
                    f(e)
            return body

        block.tensor(run(kb.E["pe"].q))
        block.scalar(run(kb.E["act"].q))
        block.vector(run(kb.E["dve"].q))
        block.gpsimd(run(kb.E["pool"].q))
        block.sync(run(kb.E["sp"].q))
        stats = {n: len(e.q) for n, e in kb.E.items()}
        stats['nops'] = stats_nops
    return nc, stats


def host_consts(hf):
    half = 64
    inv_freq = (10000.0 ** (-np.arange(half, dtype=np.float32) / half)).astype(np.float32)
    pos = np.arange(W) if hf == 1 else (np.arange(W) % 2048)
    ang = pos.astype(np.float32)[None, :] * np.concatenate([inv_freq, inv_freq])[:, None]
    cosT = np.cos(ang).astype(np.float32)
    sinT = np.sin(ang).astype(np.float32)
    ident = np.eye(128, dtype=np.float32)
    rmat = np.zeros((128, 128), np.float32)
    for dp in range(64):
        rmat[dp + 64, dp] = -1.0
        rmat[dp, dp + 64] = 1.0
    k = np.arange(128)[:, None]
    q = np.arange(128)[None, :]
    tril = (k <= q).astype(np.float32)
    cst = np.stack([ident, rmat, tril, np.zeros((128, 128), np.float32)], axis=1)
    mb_cur = np.where(k <= q, 0.0, NEG).astype(np.float32)
    mb_prev = np.where(k >= q, 0.0, NEG).astype(np.float32)
    mb_prevh = mb_prev if hf == 1 else np.full((128, 128), NEG, np.float32)
    mb = np.stack([np.concatenate([mb_prev, mb_cur], 1), np.concatenate([mb_prevh, mb_cur], 1)], axis=1)
    return cosT, sinT, np.ascontiguousarray(cst), np.ascontiguousarray(mb)


_CACHE = {}


def kernel(x, norm1_g, w_in, gmlp_ln_g, gmlp_ln_b, w_spatial, b_spatial, w_out, norm2_g, w_up, w_down, final_g):
    f = np.float32
    x = np.asarray(x, f)
    w_in = np.ascontiguousarray(np.asarray(w_in, f))
    w_out = np.ascontiguousarray(np.asarray(w_out, f))
    w_up = np.ascontiguousarray(np.asarray(w_up, f))
    w_down = np.ascontiguousarray(np.asarray(w_down, f))
    g_all = np.stack([np.asarray(norm1_g, f)[0], np.asarray(norm1_g, f)[1], np.asarray(norm2_g, f)[0],
                      np.asarray(norm2_g, f)[1], np.asarray(final_g, f)], 0)
    gT = np.ascontiguousarray(g_all.reshape(5, 16, 128).transpose(2, 0, 1))
    lng = np.asarray(gmlp_ln_g, f).reshape(2, 1024)
    lnb = np.asarray(gmlp_ln_b, f).reshape(2, 1024)
    ln = np.stack([lng, lnb], 1)
    lnbc = np.ascontiguousarray(np.broadcast_to(ln[None], (128, 2, 2, 1024)))
    bs = np.asarray(b_spatial, f)
    bs4 = np.ascontiguousarray(np.broadcast_to(np.tile(bs, (1, 1, 4))[None], (128, 2, 8, 512)))
    wsT = np.ascontiguousarray(np.asarray(w_spatial, f).transpose(3, 0, 1, 2))
    in_maps = []
    for c in range(8):
        b, hf = c // 2, c % 2
        xw = x[b] if hf == 1 else np.concatenate([x[b, :2048], x[b, :2048]], 0)
        xT = np.ascontiguousarray(xw.reshape(W, 16, 128).transpose(2, 1, 0))
        cosT, sinT, cst, mb = host_consts(hf)
        in_maps.append({"xT": xT, "w_in": w_in, "w_out": w_out, "w_up": w_up, "w_down": w_down, "gT": gT,
                        "lnbc": lnbc, "bs4": bs4, "wsT": wsT, "cosT": cosT, "sinT": sinT, "cst": cst, "mb": mb})
    if "nc" not in _CACHE:
        _CACHE["nc"] = build_program()[0]
    res = run_bass_kernel_spmd(_CACHE["nc"], in_maps, core_ids=list(range(8)))
    out = np.empty((4, SEQ, D), f)
    for c in range(8):
        b, hf = c // 2, c % 2
        yT = np.asarray(res.results[c]["yT"])
        out[b, hf * 2048:(hf + 1) * 2048] = yT.transpose(2, 1, 0).reshape(2048, D)
    return out
```
